# Optimizing a Trainium2 kernel written in Bass

```python
import math
import jax, jax.numpy as jnp
from jax import lax
import numpy as np

D_MODEL = 2048
BATCH = 1
SEQ = 8192
DEPTH = 2

D_MIX = D_MODEL
GROUP = D_MIX // 4
EPS = 1e-6

A_HEADS = 4
A_CH = GROUP // A_HEADS
A_CHUNK = 128
A_IN = 2 * GROUP

B_HEADS = 8
B_HD = GROUP // B_HEADS
B_W_RANK = 32
B_A_RANK = 32
B_G_RANK = 96
B_LNX_EPS = 64e-5
B_IN = 3 * GROUP + B_W_RANK + B_A_RANK + B_G_RANK
B_SPLITS = [GROUP, 2 * GROUP, 3 * GROUP, 3 * GROUP + B_W_RANK, 3 * GROUP + B_W_RANK + B_A_RANK]

C_HEADS = 8
C_HD = GROUP // C_HEADS
C_BLOCK = 256
C_TOPK = 3
C_QBLOCK = 128
ROPE_THETA = 10000.0
C_IN = 3 * GROUP

D_HEADS = 4
D_DK = 64
D_DV = GROUP // D_HEADS
D_GATE_RANK = 16
D_GATE_TEMP = 16.0
D_CHUNK = 64
D_IN = 2 * D_HEADS * D_DK + GROUP + D_GATE_RANK + GROUP
D_SPLITS = [D_HEADS * D_DK, 2 * D_HEADS * D_DK, 2 * D_HEADS * D_DK + GROUP, 2 * D_HEADS * D_DK + GROUP + D_GATE_RANK]

N_IN = A_IN + B_IN + C_IN + D_IN
IN_SPLITS = [A_IN, A_IN + B_IN, A_IN + B_IN + C_IN]

D_FF = 5632
CONV_W = 3

kernel_name = "hymba_style_gmlp_rwkv7_moba_gla_convffn"

F32 = jnp.float32


def _rmsnorm(x, g):
    xf = x.astype(F32)
    y = xf * lax.rsqrt(jnp.mean(xf * xf, axis=-1, keepdims=True) + EPS)
    return (y * g.astype(F32)).astype(x.dtype)


def _rope(x, pos):
    half = x.shape[-1] // 2
    inv = ROPE_THETA ** (-jnp.arange(half, dtype=F32) / half)
    ang = pos.astype(F32)[:, None] * inv[None, :]
    cos = jnp.cos(ang)[None, :, None, :]
    sin = jnp.sin(ang)[None, :, None, :]
    x1, x2 = x[..., :half], x[..., half:]
    return jnp.concatenate([x1 * cos - x2 * sin, x1 * sin + x2 * cos], axis=-1)


def _gmlp_mixer(p, ln_g, ln_b, ws, bs):
    b, s, _ = p.shape
    z = jax.nn.gelu(p.astype(F32))
    u, v = jnp.split(z, 2, axis=-1)
    mu = jnp.mean(v, axis=-1, keepdims=True)
    var = jnp.mean(jnp.square(v - mu), axis=-1, keepdims=True)
    v = (v - mu) * lax.rsqrt(var + EPS) * ln_g + ln_b
    v = v.reshape(b, s // A_CHUNK, A_CHUNK, A_HEADS, A_CH)
    causal = jnp.tril(jnp.ones((A_CHUNK, A_CHUNK), dtype=bool))
    w_s = jnp.where(causal[None], ws.astype(F32), 0.0)
    mixed = jnp.einsum("hts,bnshc->bnthc", w_s, v) + bs.astype(F32).T[None, None, :, :, None]
    return (u * mixed.reshape(b, s, GROUP)).astype(p.dtype)


def _rwkv7_mixer(p, mu, w0, w2, a0, a2, g2, k_k, k_a, r_k, lnx_g, lnx_b):
    b, s, _ = p.shape
    dtype = p.dtype
    p = p.astype(F32)
    prev = jnp.pad(p, ((0, 0), (1, 0), (0, 0)))[:, :s]
    p = p + (prev - p) * mu
    r, k, v, xw, xa, xg = jnp.split(p, B_SPLITS, axis=-1)
    w = -jax.nn.softplus(-(w0 + jnp.tanh(xw) @ w2)) - 0.5
    decay = jnp.exp(-jnp.exp(w))
    a = jax.nn.sigmoid(a0 + xa @ a2)
    g = jax.nn.sigmoid(xg) @ g2

    def heads(t):
        return t.reshape(b, s, B_HEADS, B_HD)

    kk = heads(k * k_k)
    kk = kk / jnp.maximum(jnp.linalg.norm(kk, axis=-1, keepdims=True), 1e-12)
    k = k * (1.0 + (a - 1.0) * k_a)
    r_h, k_h, v_h, a_h, d_h = heads(r), heads(k), heads(v), heads(a), heads(decay)
    b_h = kk * a_h
    xs = tuple(jnp.moveaxis(t, 1, 0) for t in (r_h, d_h, k_h, v_h, kk, b_h))

    def step(state, inp):
        r_t, d_t, k_t, v_t, kk_t, b_t = inp
        sa = jnp.einsum("bhvk,bhk->bhv", state, -kk_t)
        state = (state * d_t[:, :, None, :] + sa[..., None] * b_t[:, :, None, :]
                 + v_t[..., None] * k_t[:, :, None, :])
        return state, jnp.einsum("bhvk,bhk->bhv", state, r_t)

    state0 = jnp.zeros((b, B_HEADS, B_HD, B_HD), F32)
    _, y = lax.scan(step, state0, xs)
    y = jnp.moveaxis(y, 0, 1)
    ym = jnp.mean(y, axis=-1, keepdims=True)
    yv = jnp.mean(jnp.square(y - ym), axis=-1, keepdims=True)
    y = ((y - ym) * lax.rsqrt(yv + B_LNX_EPS)).reshape(b, s, GROUP) * lnx_g + lnx_b
    bonus = jnp.sum(r_h * k_h * r_k, axis=-1, keepdims=True) * v_h
    y = y + bonus.reshape(b, s, GROUP)
    return (y * g).astype(dtype)


def _moba_mixer(p):
    b, s, _ = p.shape
    dtype = p.dtype
    p = p.astype(F32)
    q, k, v = jnp.split(p, 3, axis=-1)
    shape = (b, s, C_HEADS, C_HD)
    pos = jnp.arange(s)
    q = _rope(q.reshape(shape), pos) * (C_HD ** -0.5)
    k = _rope(k.reshape(shape), pos)
    v = v.reshape(shape)
    nb = -(-s // C_BLOCK)
    pad = nb * C_BLOCK - s

    def blocks(t):
        t = jnp.pad(t, ((0, 0), (0, pad), (0, 0), (0, 0)))
        return t.reshape(b, nb, C_BLOCK, C_HEADS, C_HD).transpose(0, 3, 1, 2, 4)

    kb, vb = blocks(k), blocks(v)
    kmean = jnp.mean(kb, axis=3)
    qt = q.transpose(0, 2, 1, 3)
    bscore = jnp.einsum("bhsd,bhnd->bhsn", qt, kmean)
    fully_past = jnp.arange(nb)[None, :] < (pos // C_BLOCK)[:, None]
    bscore = jnp.where(fully_past, bscore, -jnp.inf)
    n_sel = min(C_TOPK, nb)
    top_s, top_i = lax.top_k(bscore, n_sel)
    top_ok = jnp.isfinite(top_s)
    gather = jax.vmap(jax.vmap(lambda blk, idx: blk[idx]))
    own_off = jnp.arange(C_BLOCK)
    q_off = jnp.arange(C_QBLOCK)

    def attend(qi):
        start = qi * C_QBLOCK
        q_blk = lax.dynamic_slice_in_dim(qt, start, C_QBLOCK, axis=2)
        idx = lax.dynamic_slice_in_dim(top_i, start, C_QBLOCK, axis=2)
        ok = lax.dynamic_slice_in_dim(top_ok, start, C_QBLOCK, axis=2)
        k_sel = gather(kb, idx)
        v_sel = gather(vb, idx)
        s_sel = jnp.einsum("bhqd,bhqkld->bhqkl", q_blk, k_sel)
        s_sel = jnp.where(ok[..., None], s_sel, -jnp.inf).reshape(b, C_HEADS, C_QBLOCK, n_sel * C_BLOCK)
        own = start // C_BLOCK
        k_own = lax.dynamic_index_in_dim(kb, own, axis=2, keepdims=False)
        v_own = lax.dynamic_index_in_dim(vb, own, axis=2, keepdims=False)
        s_own = jnp.einsum("bhqd,bhld->bhql", q_blk, k_own)
        visible = (own * C_BLOCK + own_off)[None, :] <= (start + q_off)[:, None]
        s_own = jnp.where(visible, s_own, -jnp.inf)
        probs = jax.nn.softmax(jnp.concatenate([s_sel, s_own], axis=-1), axis=-1)
        p_sel = probs[..., :n_sel * C_BLOCK].reshape(b, C_HEADS, C_QBLOCK, n_sel, C_BLOCK)
        p_own = probs[..., n_sel * C_BLOCK:]
        return (jnp.einsum("bhqkl,bhqkld->bhqd", p_sel, v_sel)
                + jnp.einsum("bhql,bhld->bhqd", p_own, v_own))

    out = lax.map(attend, jnp.arange(s // C_QBLOCK))
    out = out.transpose(1, 0, 3, 2, 4).reshape(b, s, GROUP)
    return out.astype(dtype)


def _gla_mixer(p, gate_w2, gate_b, norm_g):
    b, s, _ = p.shape
    dtype = p.dtype
    p = p.astype(F32)
    q, k, v, xg, og = jnp.split(p, D_SPLITS, axis=-1)
    log_a = jax.nn.log_sigmoid(xg @ gate_w2 + gate_b) / D_GATE_TEMP
    nc = s // D_CHUNK

    def chunks(t, d):
        return t.reshape(b, nc, D_CHUNK, D_HEADS, d).transpose(1, 0, 3, 2, 4)

    qc = chunks(q * (D_DK ** -0.5), D_DK)
    kc = chunks(k, D_DK)
    vc = chunks(v, D_DV)
    gc = chunks(log_a, D_DK)
    causal = jnp.tril(jnp.ones((D_CHUNK, D_CHUNK), dtype=bool))

    def step(state, inp):
        q_c, k_c, v_c, g_c = inp
        cum = jnp.cumsum(g_c, axis=2)
        diff = cum[:, :, :, None, :] - cum[:, :, None, :, :]
        dec = jnp.exp(jnp.where(causal[:, :, None], diff, -jnp.inf))
        scores = jnp.einsum("bhtd,bhsd,bhtsd->bhts", q_c, k_c, dec)
        o = scores @ v_c + jnp.einsum("bhtd,bhde->bhte", q_c * jnp.exp(cum), state)
        last = cum[:, :, -1, :]
        state = (jnp.exp(last)[..., None] * state
                 + jnp.einsum("bhsd,bhse->bhde", k_c * jnp.exp(last[:, :, None, :] - cum), v_c))
        return state, o

    state0 = jnp.zeros((b, D_HEADS, D_DK, D_DV), F32)
    _, o = lax.scan(step, state0, (qc, kc, vc, gc))
    o = o.transpose(1, 0, 3, 2, 4).reshape(b, s, D_HEADS, D_DV)
    o = o * lax.rsqrt(jnp.mean(o * o, axis=-1, keepdims=True) + EPS) * norm_g
    return (o.reshape(b, s, GROUP) * jax.nn.silu(og)).astype(dtype)


def _conv_ffn(x, w_up, conv_w, conv_b, w_down):
    s = x.shape[1]
    h = x @ w_up
    hp = jnp.pad(h, ((0, 0), (CONV_W - 1, 0), (0, 0)))
    acc = conv_b
    for j in range(CONV_W):
        acc = acc + hp[:, j:j + s] * conv_w[j]
    gate, up = jnp.split(acc, 2, axis=-1)
    return (jax.nn.silu(gate) * up) @ w_down


def setup_inputs(seed: int = 0) -> dict:
    key = jax.random.key(seed)
    ks = iter(jax.random.split(key, 32))

    def nrm(shape, scale):
        return jax.random.normal(next(ks), shape, F32) * scale

    def gain(shape):
        return 1.0 + nrm(shape, 0.02)

    L = DEPTH
    return {
        "x": nrm((BATCH, SEQ, D_MODEL), 1.0),
        "mix_norm_g": gain((L, D_MODEL)),
        "w_in": nrm((L, D_MODEL, N_IN), D_MODEL ** -0.5),
        "a_ln_g": gain((L, GROUP)),
        "a_ln_b": nrm((L, GROUP), 0.02),
        "a_ws": nrm((L, A_HEADS, A_CHUNK, A_CHUNK), A_CHUNK ** -0.5),
        "a_bs": 1.0 + nrm((L, A_HEADS, A_CHUNK), 0.1),
        "b_mu": jax.random.uniform(next(ks), (L, B_IN), F32),
        "b_w0": jax.random.uniform(next(ks), (L, GROUP), F32, -6.0, -1.0),
        "b_w2": nrm((L, B_W_RANK, GROUP), 0.1),
        "b_a0": nrm((L, GROUP), 0.1),
        "b_a2": nrm((L, B_A_RANK, GROUP), 0.1),
        "b_g2": nrm((L, B_G_RANK, GROUP), B_G_RANK ** -0.5),
        "b_kk": 0.85 + nrm((L, GROUP), 0.05),
        "b_ka": 1.0 + nrm((L, GROUP), 0.05),
        "b_rk": nrm((L, B_HEADS, B_HD), 0.1),
        "b_lnx_g": gain((L, GROUP)),
        "b_lnx_b": nrm((L, GROUP), 0.02),
        "d_gate_w2": nrm((L, D_GATE_RANK, D_HEADS * D_DK), D_GATE_RANK ** -0.5),
        "d_gate_b": 2.0 + nrm((L, D_HEADS * D_DK), 0.5),
        "d_norm_g": gain((L, D_DV)),
        "w_out": nrm((L, D_MIX, D_MODEL), D_MIX ** -0.5),
        "ffn_norm_g": gain((L, D_MODEL)),
        "w_up": nrm((L, D_MODEL, 2 * D_FF), D_MODEL ** -0.5),
        "conv_w": nrm((L, CONV_W, 2 * D_FF), CONV_W ** -0.5),
        "conv_b": nrm((L, 2 * D_FF), 0.02),
        "w_down": nrm((L, D_FF, D_MODEL), D_FF ** -0.5),
        "final_norm_g": gain((D_MODEL,)),
    }


def reference(x, mix_norm_g, w_in, a_ln_g, a_ln_b, a_ws, a_bs, b_mu, b_w0, b_w2, b_a0, b_a2,
              b_g2, b_kk, b_ka, b_rk, b_lnx_g, b_lnx_b, d_gate_w2, d_gate_b, d_norm_g, w_out,
              ffn_norm_g, w_up, conv_w, conv_b, w_down, final_norm_g):
    for l in range(DEPTH):
        h = _rmsnorm(x, mix_norm_g[l])
        p = h @ w_in[l]
        pa, pb, pc, pd = jnp.split(p, IN_SPLITS, axis=-1)
        ya = _gmlp_mixer(pa, a_ln_g[l], a_ln_b[l], a_ws[l], a_bs[l])
        yb = _rwkv7_mixer(pb, b_mu[l], b_w0[l], b_w2[l], b_a0[l], b_a2[l], b_g2[l],
                          b_kk[l], b_ka[l], b_rk[l], b_lnx_g[l], b_lnx_b[l])
        yc = _moba_mixer(pc)
        yd = _gla_mixer(pd, d_gate_w2[l], d_gate_b[l], d_norm_g[l])
        x = x + jnp.concatenate([ya, yb, yc, yd], axis=-1) @ w_out[l]
        x = x + _conv_ffn(_rmsnorm(x, ffn_norm_g[l]), w_up[l], conv_w[l], conv_b[l], w_down[l])
    return _rmsnorm(x, final_norm_g)
```

```python
import numpy as np
from contextlib import ExitStack
import concourse.bass as bass
import concourse.mybir as mybir
from concourse.bass_utils import run_bass_kernel_spmd

F32 = mybir.dt.float32
BF16 = mybir.dt.bfloat16
AF = mybir.ActivationFunctionType
ALU = mybir.AluOpType
AX = mybir.AxisListType

NCORES = 8
D = 2048
SEQ = 8192
DEPTH = 2
GROUP = 512
N_IN = 5808
D_FF = 5632
EPS = 1e-6


class Buf:
    __slots__ = ("name", "w", "r", "sem", "semv")

    def __init__(self, name):
        self.name = name
        self.w = None
        self.r = {}
        self.sem = None
        self.semv = 0


class Tile:
    def __init__(self, t, b):
        self.t = t
        self.b = b

    def __getitem__(self, idx):
        return self.t[idx]


class K:
    def __init__(self):
        self.nc = bass.Bass("TRN2", target_bir_lowering=False)
        self.es = ExitStack()
        nc = self.nc
        self.engs = dict(pe=nc.tensor, dve=nc.vector, act=nc.scalar, pool=nc.gpsimd, sp=nc.sync)
        self.esem = {}
        self.ecnt = {}
        self.seen = {}
        for e in self.engs:
            self.esem[e] = self.es.enter_context(nc.semaphore("sem_" + e))
            self.ecnt[e] = 0
            self.seen[e] = {}
        self.nsem = 0
        self.out_tokens = []
        self.uid = 0

    def dram(self, name, shape, dt, kind):
        return self.nc.dram_tensor(name, list(shape), dt, kind=kind).ap()

    def sb(self, name, shape, dt=F32):
        t = self.es.enter_context(self.nc.sbuf_tensor(name, list(shape), dt))
        return Tile(t, Buf(name))

    def ps(self, name, shape, dt=F32):
        t = self.es.enter_context(self.nc.psum_tensor(name, list(shape), dt))
        return Tile(t, Buf(name))

    def buf(self, name):
        return Buf(name)

    def _wait(self, eng, tok):
        if tok is None:
            return
        key, sem, val = tok
        if key == eng and eng == "pe":
            return
        if self.seen[eng].get(key, 0) >= val:
            return
        self.engs[eng].wait_ge(sem, val)
        self.seen[eng][key] = val

    def _deps(self, eng, reads, writes, dma_owner=None):
        for b in reads:
            self._wait(eng, b.w)
        for b in writes:
            if not (dma_owner is not None and b.w is not None and b.w[0] == "d%d" % id(dma_owner)):
                self._wait(eng, b.w)
            for tok in b.r.values():
                self._wait(eng, tok)

    def _commit(self, tok, reads, writes):
        for b in reads:
            b.r[tok[0]] = tok
        for b in writes:
            b.w = tok
            b.r = {}

    def op(self, eng, fn, reads=(), writes=()):
        reads = [x.b if isinstance(x, Tile) else x for x in reads]
        writes = [x.b if isinstance(x, Tile) else x for x in writes]
        self._deps(eng, reads, writes)
        ins = fn(self.engs[eng])
        self.ecnt[eng] += 1
        ins.then_inc(self.esem[eng], 1)
        tok = (eng, self.esem[eng], self.ecnt[eng])
        self._commit(tok, reads, writes)
        return tok

    def dma(self, q, out, in_, reads=(), writes=(), is_output=False):
        reads = [x.b if isinstance(x, Tile) else x for x in reads]
        writes = [x.b if isinstance(x, Tile) else x for x in writes]
        owner = (writes + reads)[0] if (writes or reads) else Buf("anon")
        self._deps(q, reads, writes, dma_owner=owner)
        if owner.sem is None:
            self.nsem += 1
            owner.sem = self.es.enter_context(self.nc.semaphore("dsem%d" % self.nsem))
        owner.semv += 16
        self.engs[q].dma_start(out=out, in_=in_).then_inc(owner.sem, 16)
        tok = ("d%d" % id(owner), owner.sem, owner.semv)
        self._commit(tok, reads, writes)
        if is_output:
            self.out_tokens.append(tok)
        return tok

    def finish(self):
        for tok in self.out_tokens:
            self._wait("sp", tok)
        self.es.close()
        return self.nc


def mm(k, out, lhsT, rhs, start, stop, reads, writes):
    return k.op("pe", lambda e: e.matmul(out, lhsT, rhs, start=start, stop=stop), reads, writes)


def rmsnorm_fm(k, xt, KO, T, g_t, ones_t, ht, tag, pss):
    nc = k.nc
    rstd, sq = norm_scratch(k)
    nt = (T + 511) // 512
    for n in range(nt):
        c0 = n * 512
        cw = min(512, T - c0)
        ps = pss[n % len(pss)]
        for ko in range(KO):
            s = sq[ko % 2]
            k.op("act", lambda e: e.activation(out=s[:, 0:cw], in_=xt[:, ko, c0:c0 + cw], func=AF.Square),
                 [xt], [s])
            mm(k, ps[:, 0:cw], ones_t[:, :], s[:, 0:cw], ko == 0, ko == KO - 1, [s, ones_t], [ps])
        k.op("act", lambda e: e.activation(out=rstd[:, c0:c0 + cw], in_=ps[:, 0:cw], func=AF.Sqrt,
                                           bias=eps_tile(k)[:, 0:1], scale=1.0 / (KO * 128)),
             [ps, eps_tile(k)], [rstd])
        k.op("dve", lambda e: e.reciprocal(out=rstd[:, c0:c0 + cw], in_=rstd[:, c0:c0 + cw]), [rstd], [rstd])
    for ko in range(KO):
        k.op("dve", lambda e: e.scalar_tensor_tensor(out=ht[:, ko, :], in0=xt[:, ko, :], scalar=g_t[:, ko:ko + 1],
                                                     in1=rstd[:, 0:T], op0=ALU.mult, op1=ALU.mult),
             [xt, g_t, rstd], [ht])
    return rstd


def norm_scratch(k):
    if not hasattr(k, "_nscr"):
        k._nscr = (k.sb("n_rstd", [128, 1028], F32), [k.sb("n_sq%d" % i, [128, 512], F32) for i in range(2)])
    return k._nscr


def eps_tile(k):
    if not hasattr(k, "_eps"):
        k._eps = k.sb("eps_c", [128, 1], F32)
        k.op("dve", lambda e: e.memset(k._eps[:, :], EPS), [], [k._eps])
    return k._eps


def ones_tile(k):
    if not hasattr(k, "_ones"):
        k._ones = k.sb("ones_c", [128, 128], F32)
        k.op("dve", lambda e: e.memset(k._ones[:, :], 1.0), [], [k._ones])
    return k._ones


def build_A(N=N_IN, T=1024):
    k = K()
    KO = D // 128
    xT = k.dram("xT", [D, T], F32, "ExternalInput")
    g = k.dram("g", [128, KO], F32, "ExternalInput")
    w = k.dram("w", [D, N], F32, "ExternalInput")
    pT = k.dram("pT", [N, T], F32, "ExternalOutput")
    xt = k.sb("xt", [128, KO, T], F32)
    ht = k.sb("ht", [128, KO, T], BF16)
    g_t = k.sb("g_t", [128, KO], F32)
    ones_t = ones_tile(k)
    pss = [k.ps("ps%d" % i, [128, 512], F32) for i in range(4)]
    k.dma("sp", g_t[:, :], g, [], [g_t])
    xTv = xT.rearrange("(ko p) t -> p ko t", p=128)
    for ko in range(KO):
        k.dma("sp", xt[:, ko, :], xTv[:, ko, :], [], [xt])
    rmsnorm_fm(k, xt, KO, T, g_t, ones_t, ht, "nA", pss)
    CG = 512
    ngr = (N + CG - 1) // CG
    wb = [k.sb("wb%d" % i, [128, KO, CG], BF16) for i in range(2)]
    ob = [k.sb("ob%d" % i, [128, T], F32) for i in range(2)]
    wv = w.rearrange("(ko p) n -> p ko n", p=128)
    oi = 0
    pi = 0
    for gi in range(ngr):
        n0 = gi * CG
        nw = min(CG, N - n0)
        wt = wb[gi % 2]
        for ko in range(KO):
            k.dma("pool", wt[:, ko, 0:nw], wv[:, ko, n0:n0 + nw], [], [wt])
        for j0 in range(0, nw, 128):
            jw = min(128, nw - j0)
            o = ob[oi % 2]
            oi += 1
            for n in range(T // 512):
                ps = pss[pi % 4]
                pi += 1
                for ko in range(KO):
                    mm(k, ps[0:jw, :], wt[:, ko, j0:j0 + jw], ht[:, ko, n * 512:(n + 1) * 512],
                       ko == 0, ko == KO - 1, [wt, ht], [ps])
                if n % 2 == 0:
                    k.op("act", lambda e: e.copy(out=o[0:jw, n * 512:(n + 1) * 512], in_=ps[0:jw, :]), [ps], [o])
                else:
                    k.op("dve", lambda e: e.tensor_copy(out=o[0:jw, n * 512:(n + 1) * 512], in_=ps[0:jw, :]),
                         [ps], [o])
            k.dma("sp", pT[n0 + j0:n0 + j0 + jw, :], o[0:jw, :], [o], [], is_output=True)
    return k.finish()


def build_C(final, NU=2, U=512):
    k = K()
    KO = D // 128
    UH = U + 2
    NH = D_FF // 128
    xT = k.dram("xT", [NU, D, UH], F32, "ExternalInput")
    yT = k.dram("yT", [NU, D, UH], F32, "ExternalInput")
    wo = k.dram("wo", [D, D], F32, "ExternalInput")
    g2 = k.dram("g2", [128, KO], F32, "ExternalInput")
    wu = k.dram("wu", [D, 2 * D_FF], F32, "ExternalInput")
    cw = k.dram("cw", [128, 2 * NH, 3], F32, "ExternalInput")
    cb = k.dram("cb", [128, 2 * NH], F32, "ExternalInput")
    wd = k.dram("wd", [D_FF, D], F32, "ExternalInput")
    gf = k.dram("gf", [128, KO], F32, "ExternalInput")
    xo = k.dram("xo", [NU, D, U], F32, "ExternalOutput")

    xt = k.sb("xt", [128, KO, UH], F32)
    yt = k.sb("yt", [128, KO, UH], BF16)
    at = k.sb("at", [128, NH, U], BF16)
    g2_t = k.sb("g2_t", [128, KO], F32)
    gf_t = k.sb("gf_t", [128, KO], F32)
    cw_t = k.sb("cw_t", [128, 2 * NH, 3], F32)
    cb_t = k.sb("cb_t", [128, 2 * NH], F32)
    ones_t = ones_tile(k)
    pss = [k.ps("ps%d" % i, [128, 512], F32) for i in range(8)]
    k.dma("sp", g2_t[:, :], g2, [], [g2_t])
    k.dma("sp", gf_t[:, :], gf, [], [gf_t])
    k.dma("sp", cw_t[:, :, :], cw, [], [cw_t])
    k.dma("sp", cb_t[:, :], cb, [], [cb_t])
    wob = [k.sb("wob%d" % i, [128, KO, 256], BF16) for i in range(2)]
    wub = [k.sb("wub%d" % i, [128, KO, 2, 256], BF16) for i in range(2)]
    wdb = [k.sb("wdb%d" % i, [128, NH, 128], BF16) for i in range(2)]
    sc = [[k.sb("sc%d_%d" % (i, j), [128, UH], F32) for j in range(4)] for i in range(2)]
    sg = [k.sb("sg%d" % i, [128, U], F32) for i in range(2)]
    wov = wo.rearrange("(ko p) n -> p ko n", p=128)
    wuv = wu.rearrange("(ko p) n -> p ko n", p=128)
    wdv = wd.rearrange("(j p) n -> p j n", p=128)
    cnt = dict(wo=0, wu=0, wd=0, sc=0)

    for u in range(NU):
        xv = xT[u].rearrange("(ko p) t -> p ko t", p=128)
        yv = yT[u].rearrange("(ko p) t -> p ko t", p=128)
        for ko in range(KO):
            k.dma("sp", xt[:, ko, :], xv[:, ko, :], [], [xt])
            k.dma("pool", yt[:, ko, :], yv[:, ko, :], [], [yt])
        for gi in range(D // 256):
            wt = wob[cnt["wo"] % 2]
            cnt["wo"] += 1
            for ko in range(KO):
                k.dma("pool", wt[:, ko, :], wov[:, ko, gi * 256:(gi + 1) * 256], [], [wt])
            for jj in range(2):
                dj = gi * 2 + jj
                pm, ph = pss[(dj % 2) * 2], pss[(dj % 2) * 2 + 1]
                for ko in range(KO):
                    mm(k, pm[:, 0:U], wt[:, ko, jj * 128:(jj + 1) * 128], yt[:, ko, 2:UH], ko == 0, ko == KO - 1,
                       [wt, yt], [pm])
                for ko in range(KO):
                    mm(k, ph[:, 0:2], wt[:, ko, jj * 128:(jj + 1) * 128], yt[:, ko, 0:2], ko == 0, ko == KO - 1,
                       [wt, yt], [ph])
                k.op("dve", lambda e: e.tensor_tensor(out=xt[:, dj, 2:UH], in0=pm[:, 0:U], in1=xt[:, dj, 2:UH],
                                                      op=ALU.add), [pm, xt], [xt])
                k.op("dve", lambda e: e.tensor_tensor(out=xt[:, dj, 0:2], in0=ph[:, 0:2], in1=xt[:, dj, 0:2],
                                                      op=ALU.add), [ph, xt], [xt])
        rmsnorm_fm(k, xt, KO, UH, g2_t, ones_t, yt, "nC%d" % u, pss[4:6])
        for j2 in range(NH // 2):
            wt = wub[cnt["wu"] % 2]
            cnt["wu"] += 1
            for ko in range(KO):
                k.dma("pool", wt[:, ko, 0, :], wuv[:, ko, j2 * 256:(j2 + 1) * 256], [], [wt])
                k.dma("pool", wt[:, ko, 1, :], wuv[:, ko, D_FF + j2 * 256:D_FF + (j2 + 1) * 256], [], [wt])
            for jj in range(2):
                j = j2 * 2 + jj
                s = sc[cnt["sc"] % 2]
                sgt = sg[cnt["sc"] % 2]
                pb = (cnt["sc"] % 2) * 4
                cnt["sc"] += 1
                for half in range(2):
                    pm, ph = pss[pb + half * 2], pss[pb + half * 2 + 1]
                    raw, acc = s[half * 2], s[half * 2 + 1]
                    ch = half * NH + j
                    for ko in range(KO):
                        mm(k, pm[:, 0:U], wt[:, ko, half, jj * 128:(jj + 1) * 128], yt[:, ko, 2:UH],
                           ko == 0, ko == KO - 1, [wt, yt], [pm])
                    for ko in range(KO):
                        mm(k, ph[:, 0:2], wt[:, ko, half, jj * 128:(jj + 1) * 128], yt[:, ko, 0:2],
                           ko == 0, ko == KO - 1, [wt, yt], [ph])
                    k.op("act", lambda e: e.copy(out=raw[:, 2:UH], in_=pm[:, 0:U]), [pm], [raw])
                    k.op("act", lambda e: e.copy(out=raw[:, 0:2], in_=ph[:, 0:2]), [ph], [raw])
                    k.op("dve", lambda e: e.tensor_scalar(out=acc[:, 0:U], in0=raw[:, 2:UH],
                                                          scalar1=cw_t[:, ch, 2:3], scalar2=cb_t[:, ch:ch + 1],
                                                          op0=ALU.mult, op1=ALU.add), [raw, cw_t, cb_t], [acc])
                    k.op("dve", lambda e: e.scalar_tensor_tensor(out=acc[:, 0:U], in0=raw[:, 1:U + 1],
                                                                 scalar=cw_t[:, ch, 1:2], in1=acc[:, 0:U],
                                                                 op0=ALU.mult, op1=ALU.add), [raw, cw_t, acc], [acc])
                    k.op("dve", lambda e: e.scalar_tensor_tensor(out=acc[:, 0:U], in0=raw[:, 0:U],
                                                                 scalar=cw_t[:, ch, 0:1], in1=acc[:, 0:U],
                                                                 op0=ALU.mult, op1=ALU.add), [raw, cw_t, acc], [acc])
                k.op("act", lambda e: e.activation(out=sgt[:, :], in_=s[1][:, 0:U], func=AF.Silu), [s[1]], [sgt])
                k.op("dve", lambda e: e.tensor_tensor(out=at[:, j, :], in0=sgt[:, :], in1=s[3][:, 0:U], op=ALU.mult),
                     [sgt, s[3]], [at])
        for gi in range(D // 128):
            wt = wdb[cnt["wd"] % 2]
            cnt["wd"] += 1
            for j in range(0, NH, 4):
                k.dma("pool", wt[:, j:j + 4, :], wdv[:, j:j + 4, gi * 128:(gi + 1) * 128], [], [wt])
            for jj in range(1):
                dj = gi
                pm = pss[dj % 2]
                for j in range(NH):
                    mm(k, pm[:, 0:U], wt[:, j, jj * 128:(jj + 1) * 128], at[:, j, :], j == 0, j == NH - 1,
                       [wt, at], [pm])
                k.op("dve", lambda e: e.tensor_tensor(out=xt[:, dj, 2:UH], in0=pm[:, 0:U], in1=xt[:, dj, 2:UH],
                                                      op=ALU.add), [pm, xt], [xt])
        if final:
            fin_norm(k, xt, KO, U, gf_t, ones_t, pss[4:6], "nF%d" % u)
        xov = xo[u].rearrange("(ko p) t -> p ko t", p=128)
        for ko in range(KO):
            k.dma("sp", xov[:, ko, :], xt[:, ko, 2:UH], [xt], [], is_output=True)
    return k.finish()


def fin_norm(k, xt, KO, U, g_t, ones_t, pss, tag):
    rstd, sq = norm_scratch(k)
    ps = pss[0]
    for ko in range(KO):
        s = sq[ko % 2]
        k.op("act", lambda e: e.activation(out=s[:, 0:U], in_=xt[:, ko, 2:2 + U], func=AF.Square), [xt], [s])
        mm(k, ps[:, 0:U], ones_t[:, :], s[:, 0:U], ko == 0, ko == KO - 1, [s, ones_t], [ps])
    k.op("act", lambda e: e.activation(out=rstd[:, 0:U], in_=ps[:, 0:U], func=AF.Sqrt,
                                       bias=eps_tile(k)[:, 0:1], scale=1.0 / (KO * 128)), [ps, eps_tile(k)], [rstd])
    k.op("dve", lambda e: e.reciprocal(out=rstd[:, 0:U], in_=rstd[:, 0:U]), [rstd], [rstd])
    for ko in range(KO):
        k.op("dve", lambda e: e.scalar_tensor_tensor(out=xt[:, ko, 2:2 + U], in0=xt[:, ko, 2:2 + U],
                                                     scalar=g_t[:, ko:ko + 1], in1=rstd[:, 0:U],
                                                     op0=ALU.mult, op1=ALU.mult), [xt, g_t, rstd], [xt])


def gelu_tanh(k, out, x, tmp, width, tiles, out_tile):
    k.op("act", lambda e: e.activation(out=tmp[:, 0:width], in_=x, func=AF.Square), tiles, [tmp])
    k.op("dve", lambda e: e.tensor_scalar(out=tmp[:, 0:width], in0=tmp[:, 0:width], scalar1=0.044715, scalar2=1.0,
                                          op0=ALU.mult, op1=ALU.add), [tmp], [tmp])
    k.op("dve", lambda e: e.tensor_tensor(out=tmp[:, 0:width], in0=tmp[:, 0:width], in1=x, op=ALU.mult),
         [tmp] + tiles, [tmp])
    k.op("act", lambda e: e.activation(out=tmp[:, 0:width], in_=tmp[:, 0:width], func=AF.Sigmoid,
                                       scale=1.5957691216057308), [tmp], [tmp])
    k.op("dve", lambda e: e.tensor_tensor(out=out, in0=tmp[:, 0:width], in1=x, op=ALU.mult), [tmp] + tiles,
         [out_tile])


def build_Ba(T=1024):
    k = K()
    NCH = T // 128
    puT = k.dram("puT", [512, T], F32, "ExternalInput")
    pv = k.dram("pv", [T, 512], F32, "ExternalInput")
    lng = k.dram("lng", [128, 512], F32, "ExternalInput")
    lnb = k.dram("lnb", [128, 512], F32, "ExternalInput")
    ws = k.dram("ws", [4, 128, 128], F32, "ExternalInput")
    bsb = k.dram("bsb", [128, 4, 128], F32, "ExternalInput")
    triu = k.dram("triu", [128, 128], F32, "ExternalInput")
    ident = k.dram("ident", [128, 128], F32, "ExternalInput")
    yaT = k.dram("yaT", [512, T], F32, "ExternalOutput")

    ut = k.sb("ut", [128, 4, T], F32)
    lng_t = k.sb("lng_t", [128, 512], F32)
    lnb_t = k.sb("lnb_t", [128, 512], F32)
    ws_t = k.sb("ws_t", [128, 4, 128], F32)
    wsT_t = k.sb("wsT_t", [128, 4, 128], F32)
    bsb_t = k.sb("bsb_t", [128, 4, 128], F32)
    triu_t = k.sb("triu_t", [128, 128], F32)
    id_t = k.sb("id_t", [128, 128], F32)
    tmp = k.sb("tmp", [128, 1024], F32)
    pss = [k.ps("ps%d" % i, [128, 512], F32) for i in range(4)]
    k.dma("sp", lng_t[:, :], lng, [], [lng_t])
    k.dma("sp", lnb_t[:, :], lnb, [], [lnb_t])
    k.dma("sp", ws_t[:, :, :], ws.rearrange("h t s -> t h s"), [], [ws_t])
    k.dma("sp", bsb_t[:, :, :], bsb, [], [bsb_t])
    k.dma("sp", triu_t[:, :], triu, [], [triu_t])
    k.dma("sp", id_t[:, :], ident, [], [id_t])
    k.dma("sp", ut[:, :, :], puT.rearrange("(h c) t -> c h t", c=128), [], [ut])
    for h in range(4):
        k.op("pe", lambda e: e.transpose(pss[0][:, h * 128:(h + 1) * 128], ws_t[:, h, :], id_t[:, :]),
             [ws_t, id_t], [pss[0]])
        k.op("dve", lambda e: e.tensor_tensor(out=wsT_t[:, h, :], in0=pss[0][:, h * 128:(h + 1) * 128],
                                              in1=triu_t[:, :], op=ALU.mult), [pss[0], triu_t], [wsT_t])
    for h in range(4):
        gelu_tanh(k, ut[:, h, :], ut[:, h, :], tmp, T, [ut], ut)
    vb = [k.sb("vb%d" % i, [128, 512], F32) for i in range(2)]
    vg = [k.sb("vg%d" % i, [128, 512], F32) for i in range(2)]
    st = k.sb("st", [128, 6], F32)
    mv = k.sb("mv", [128, 2], F32)
    rs = k.sb("rs", [128, 1], F32)
    yt = k.sb("yt", [128, 4, T], F32)
    mx = k.sb("mx", [128, 512], F32)
    for n in range(NCH):
        v_in, v_g = vb[n % 2], vg[n % 2]
        k.dma("sp", v_in[:, :], pv[n * 128:(n + 1) * 128, :], [], [v_in])
        gelu_tanh(k, v_g[:, :], v_in[:, :], tmp, 512, [v_in], v_g)
        k.op("dve", lambda e: e.bn_stats(out=st[:, :], in_=v_g[:, :]), [v_g], [st])
        k.op("dve", lambda e: e.bn_aggr(out=mv[:, :], in_=st[:, :]), [st], [mv])
        k.op("act", lambda e: e.activation(out=rs[:, :], in_=mv[:, 1:2], func=AF.Sqrt, bias=eps_tile(k)[:, 0:1],
                                           scale=1.0), [mv, eps_tile(k)], [rs])
        k.op("dve", lambda e: e.reciprocal(out=rs[:, :], in_=rs[:, :]), [rs], [rs])
        k.op("dve", lambda e: e.tensor_scalar(out=v_g[:, :], in0=v_g[:, :], scalar1=mv[:, 0:1], scalar2=rs[:, 0:1],
                                              op0=ALU.subtract, op1=ALU.mult), [v_g, mv, rs], [v_g])
        k.op("dve", lambda e: e.tensor_tensor(out=v_g[:, :], in0=v_g[:, :], in1=lng_t[:, :], op=ALU.mult),
             [v_g, lng_t], [v_g])
        k.op("dve", lambda e: e.tensor_tensor(out=v_g[:, :], in0=v_g[:, :], in1=lnb_t[:, :], op=ALU.add),
             [v_g, lnb_t], [v_g])
        ps = pss[1 + n % 2]
        for h in range(4):
            mm(k, ps[:, h * 128:(h + 1) * 128], v_g[:, h * 128:(h + 1) * 128], wsT_t[:, h, :], True, True,
               [v_g, wsT_t], [ps])
        k.op("dve", lambda e: e.tensor_tensor(out=mx[:, :], in0=ps[:, :],
                                              in1=bsb_t[:, :, :].rearrange("p h t -> p (h t)"), op=ALU.add),
             [ps, bsb_t], [mx])
        k.op("dve", lambda e: e.tensor_tensor(out=yt[:, :, n * 128:(n + 1) * 128],
                                              in0=mx[:, :].rearrange("p (h t) -> p h t", h=4),
                                              in1=ut[:, :, n * 128:(n + 1) * 128], op=ALU.mult), [mx, ut], [yt])
    k.dma("sp", yaT.rearrange("(h c) t -> c h t", c=128), yt[:, :, :], [yt], [], is_output=True)
    return k.finish()


def build_Bd(S=SEQ):
    k = K()
    G = 512
    NG = S // G
    qT = k.dram("qT", [64, S], F32, "ExternalInput")
    kT = k.dram("kT", [64, S], F32, "ExternalInput")
    v = k.dram("v", [S, 128], F32, "ExternalInput")
    og = k.dram("og", [S, 128], F32, "ExternalInput")
    xgT = k.dram("xgT", [16, S], F32, "ExternalInput")
    w2h = k.dram("w2h", [16, 64], F32, "ExternalInput")
    gb = k.dram("gb", [64, 1], F32, "ExternalInput")
    ngb = k.dram("ngb", [64, 128], F32, "ExternalInput")
    ident = k.dram("ident", [128, 128], F32, "ExternalInput")
    maskU = k.dram("maskU", [64, 512], F32, "ExternalInput")
    yd = k.dram("yd", [S, 128], F32, "ExternalOutput")

    w2_t = k.sb("w2_t", [16, 64], F32)
    nb_t = k.sb("nb_t", [64, 1], F32)
    ng_t = k.sb("ng_t", [64, 128], F32)
    id_t = k.sb("id_t", [128, 128], F32)
    mk_t = k.sb("mk_t", [64, 512], F32)
    one_t = k.sb("one_t", [64, 64], F32)
    for t, src in ((w2_t, w2h), (nb_t, gb), (ng_t, ngb), (id_t, ident), (mk_t, maskU)):
        k.dma("sp", t[:, :], src, [], [t])
    k.op("dve", lambda e: e.tensor_scalar(out=nb_t[:, :], in0=nb_t[:, :], scalar1=-1.0, scalar2=None, op0=ALU.mult),
         [nb_t], [nb_t])
    k.op("dve", lambda e: e.memset(one_t[:, :], 1.0), [], [one_t])
    H = [k.sb("H%d" % i, [64, 128], F32) for i in range(2)]
    k.op("dve", lambda e: e.memset(H[0][:, :], 0.0), [], [H[0]])
    hi = 0

    def dbl(name, shape):
        return [k.sb("%s%d" % (name, i), shape, F32) for i in range(2)]
    q_b, k_b, xg_b = dbl("q_b", [64, G]), dbl("k_b", [64, G]), dbl("xg_b", [16, G])
    v_b, og_b, y_b = dbl("v_b", [64, 8, 128]), dbl("og_b", [64, 8, 128]), dbl("y_b", [64, 8, 128])
    sp_b, cum_b, ep_b, em_b = dbl("sp_b", [64, G]), dbl("cum_b", [64, G]), dbl("ep_b", [64, G]), dbl("em_b", [64, G])
    khT_b, sT_b = dbl("khT_b", [64, G]), dbl("sT_b", [64, G])
    o_sb, sq_sb, tmp_sb = k.sb("o_sb", [64, 128], F32), k.sb("sq_sb", [64, 128], F32), k.sb("tmp_sb", [64, 128], F32)
    ssq, rstd = k.sb("ssq", [64, 1], F32), k.sb("rstd", [64, 1], F32)
    psZ, psT, psS = k.ps("psZ", [64, 512], F32), k.ps("psT", [64, 512], F32), k.ps("psS", [64, 512], F32)
    psO = [k.ps("psO%d" % i, [64, 128], F32) for i in range(2)]
    psX = [k.ps("psX%d" % i, [64, 128], F32) for i in range(2)]

    for g in range(NG):
        b = g % 2
        t0 = g * G
        q_g, k_g, xg_g, v_g, og_g, y_g = q_b[b], k_b[b], xg_b[b], v_b[b], og_b[b], y_b[b]
        sp, cum, ep, em, khT, sT = sp_b[b], cum_b[b], ep_b[b], em_b[b], khT_b[b], sT_b[b]
        k.dma("sp", q_g[:, :], qT[:, t0:t0 + G], [], [q_g])
        k.dma("sp", k_g[:, :], kT[:, t0:t0 + G], [], [k_g])
        k.dma("sp", xg_g[:, :], xgT[:, t0:t0 + G], [], [xg_g])
        k.dma("sp", v_g[:, :, :], v[t0:t0 + G, :].rearrange("(c p) e -> p c e", p=64), [], [v_g])
        k.dma("sp", og_g[:, :, :], og[t0:t0 + G, :].rearrange("(c p) e -> p c e", p=64), [], [og_g])
        mm(k, psZ[:, :], w2_t[:, :], xg_g[:, :], True, True, [w2_t, xg_g], [psZ])
        k.op("act", lambda e: e.activation(out=sp[:, :], in_=psZ[:, :], func=AF.Exp, bias=nb_t[:, 0:1], scale=-1.0),
             [psZ, nb_t], [sp])
        k.op("act", lambda e: e.activation(out=sp[:, :], in_=sp[:, :], func=AF.Ln, bias=one_t[:, 0:1], scale=1.0),
             [sp, one_t], [sp])
        for c in range(8):
            k.op("dve", lambda e: e.tensor_tensor_scan(out=cum[:, c * 64:(c + 1) * 64], data0=one_t[:, :],
                                                       data1=sp[:, c * 64:(c + 1) * 64], initial=0.0,
                                                       op0=ALU.mult, op1=ALU.add), [sp, one_t], [cum])
        k.op("act", lambda e: e.activation(out=ep[:, :], in_=cum[:, :], func=AF.Exp, scale=-1.0 / 16), [cum], [ep])
        k.op("act", lambda e: e.activation(out=em[:, :], in_=cum[:, :], func=AF.Exp, scale=1.0 / 16), [cum], [em])
        k.op("dve", lambda e: e.scalar_tensor_tensor(out=q_g[:, :], in0=q_g[:, :], scalar=0.125, in1=ep[:, :],
                                                     op0=ALU.mult, op1=ALU.mult), [q_g, ep], [q_g])
        k.op("dve", lambda e: e.tensor_tensor(out=k_g[:, :], in0=k_g[:, :], in1=em[:, :], op=ALU.mult),
             [k_g, em], [k_g])
        for c in range(8):
            cs = slice(c * 64, (c + 1) * 64)
            k.op("pe", lambda e: e.transpose(psT[:, cs], k_g[:, cs], id_t[0:64, 0:64]), [k_g, id_t], [psT])
        k.op("act", lambda e: e.copy(out=khT[:, :], in_=psT[:, :]), [psT], [khT])
        for c in range(8):
            cs = slice(c * 64, (c + 1) * 64)
            mm(k, psS[:, cs], k_g[:, cs], q_g[:, cs], True, True, [k_g, q_g], [psS])
        k.op("dve", lambda e: e.tensor_tensor(out=sT[:, :], in0=psS[:, :], in1=mk_t[:, :], op=ALU.mult),
             [psS, mk_t], [sT])
        k.op("act", lambda e: e.activation(out=og_g[:, :, :], in_=og_g[:, :, :], func=AF.Silu), [og_g], [og_g])
        for c in range(8):
            cs = slice(c * 64, (c + 1) * 64)
            Hc, Hn = H[hi % 2], H[(hi + 1) % 2]
            pO, pX = psO[hi % 2], psX[hi % 2]
            hi += 1
            mm(k, pO[:, :], sT[:, cs], v_g[:, c, :], True, False, [sT, v_g], [pO])
            mm(k, pO[:, :], q_g[:, cs], Hc[:, :], False, True, [q_g, Hc], [pO])
            mm(k, pX[:, :], khT[:, cs], v_g[:, c, :], True, False, [khT, v_g], [pX])
            mm(k, pX[:, :], id_t[0:64, 0:64], Hc[:, :], False, True, [id_t, Hc], [pX])
            k.op("act", lambda e: e.activation(out=Hn[:, :], in_=pX[:, :], func=AF.Copy,
                                               scale=ep[:, c * 64 + 63:c * 64 + 64]), [pX, ep], [Hn])
            k.op("act", lambda e: e.copy(out=o_sb[:, :], in_=pO[:, :]), [pO], [o_sb])
            k.op("act", lambda e: e.activation(out=sq_sb[:, :], in_=pO[:, :], func=AF.Square), [pO], [sq_sb])
            k.op("dve", lambda e: e.reduce_sum(out=ssq[:, :], in_=sq_sb[:, :], axis=AX.X), [sq_sb], [ssq])
            k.op("act", lambda e: e.activation(out=rstd[:, :], in_=ssq[:, :], func=AF.Sqrt, bias=eps_tile(k)[0:64, 0:1],
                                               scale=1.0 / 128), [ssq, eps_tile(k)], [rstd])
            k.op("dve", lambda e: e.reciprocal(out=rstd[:, :], in_=rstd[:, :]), [rstd], [rstd])
            k.op("dve", lambda e: e.scalar_tensor_tensor(out=tmp_sb[:, :], in0=o_sb[:, :], scalar=rstd[:, 0:1],
                                                         in1=ng_t[:, :], op0=ALU.mult, op1=ALU.mult),
                 [o_sb, rstd, ng_t], [tmp_sb])
            k.op("dve", lambda e: e.tensor_tensor(out=y_g[:, c, :], in0=tmp_sb[:, :], in1=og_g[:, c, :], op=ALU.mult),
                 [tmp_sb, og_g], [y_g])
        k.dma("sp", yd[t0:t0 + G, :].rearrange("(c p) e -> p c e", p=64), y_g[:, :, :], [y_g], [], is_output=True)
    return k.finish()


def build_Bc(S=SEQ):
    k = K()
    BIG = 32768.0
    NQC, NKT, NB = S // 512, S // 128, S // 256
    qA = k.dram("qA", [64, S], F32, "ExternalInput")
    qB = k.dram("qB", [64, S], F32, "ExternalInput")
    kA = k.dram("kA", [64, S], F32, "ExternalInput")
    kB = k.dram("kB", [64, S], F32, "ExternalInput")
    cosT = k.dram("cosT", [64, S], F32, "ExternalInput")
    sinT = k.dram("sinT", [64, S], F32, "ExternalInput")
    v = k.dram("v", [S, 64], F32, "ExternalInput")
    blk = k.dram("blk", [32, S], F32, "ExternalInput")
    cmask = k.dram("cmask", [4, 128, 512], F32, "ExternalInput")
    ident = k.dram("ident", [128, 128], F32, "ExternalInput")
    ycT = k.dram("ycT", [64, S], F32, "ExternalOutput")

    qaug = k.sb("qaug", [96, S], BF16)
    kaug = k.sb("kaug", [96, S], BF16)
    qf = k.sb("qf", [64, S], F32)
    kmean = k.sb("kmean", [64, NB], F32)
    vaug = k.sb("vaug", [128, NKT, 65], BF16)
    id_t = k.sb("id_t", [128, 128], F32)
    cm_t = k.sb("cm_t", [128, 4, 512], F32)
    sel_t = k.sb("sel_t", [65, 64], F32)
    k.dma("sp", id_t[:, :], ident, [], [id_t])
    k.dma("sp", cm_t[:, :, :], cmask.rearrange("a p t -> p a t"), [], [cm_t])
    k.dma("pool", kaug[64:96, :], blk, [], [kaug])
    k.dma("pool", vaug[:, :, 0:64], v.rearrange("(t p) d -> p t d", p=128), [], [vaug])
    k.op("dve", lambda e: e.memset(vaug[:, :, 64:65], 1.0), [], [vaug])
    k.op("dve", lambda e: e.memset(sel_t[:, :], 0.0), [], [sel_t])
    k.op("dve", lambda e: e.memset(sel_t[64:65, :], 1.0), [], [sel_t])
    P = 1024
    ab = [[k.sb("rp%d_%d" % (i, j), [64, P], F32) for j in range(4)] for i in range(2)]
    for pc in range(S // P):
        sl = slice(pc * P, (pc + 1) * P)
        A, B, C, Sn = ab[pc % 2]
        k.dma("sp", C[:, :], cosT[:, sl], [], [C])
        k.dma("sp", Sn[:, :], sinT[:, sl], [], [Sn])
        for which in range(2):
            srcA, srcB = (qA, qB) if which == 0 else (kA, kB)
            sc = 0.125 if which == 0 else 1.0
            k.dma("sp", A[:, :], srcA[:, sl], [], [A])
            k.dma("sp", B[:, :], srcB[:, sl], [], [B])
            k.op("dve", lambda e: e.scalar_tensor_tensor(out=A[:, :], in0=A[:, :], scalar=sc, in1=C[:, :],
                                                         op0=ALU.mult, op1=ALU.mult), [A, C], [A])
            k.op("dve", lambda e: e.scalar_tensor_tensor(out=B[:, :], in0=B[:, :], scalar=sc, in1=Sn[:, :],
                                                         op0=ALU.mult, op1=ALU.mult), [B, Sn], [B])
            if which == 0:
                k.op("dve", lambda e: e.tensor_tensor(out=qf[:, sl], in0=A[:, :], in1=B[:, :], op=ALU.add),
                     [A, B], [qf])
                k.op("act", lambda e: e.copy(out=qaug[0:64, sl], in_=qf[:, sl]), [qf], [qaug])
            else:
                k.op("dve", lambda e: e.tensor_tensor(out=A[:, :], in0=A[:, :], in1=B[:, :], op=ALU.add),
                     [A, B], [A])
                k.op("act", lambda e: e.copy(out=kaug[0:64, sl], in_=A[:, :]), [A], [kaug])
                k.op("dve", lambda e: e.reduce_sum(out=kmean[:, pc * 4:(pc + 1) * 4],
                                                   in_=A[:, :].rearrange("p (b t) -> p b t", t=256), axis=AX.X),
                     [A], [kmean])
    k.op("dve", lambda e: e.tensor_scalar(out=kmean[:, :], in0=kmean[:, :], scalar1=1.0 / 256, scalar2=None,
                                          op0=ALU.mult), [kmean], [kmean])
    psB = k.ps("psB", [128, 128], F32)
    psP = k.ps("psP", [128, 512], F32)
    tts = [k.sb("tt%d" % i, [128, 96], F32) for i in range(8)]
    for t in tts:
        k.op("dve", lambda e: e.memset(t[:, 0:64], 0.0), [], [t])
    bsv = [k.sb("bsv%d" % i, [128, 32], F32) for i in range(2)]
    m8 = [k.sb("m8_%d" % i, [128, 8], F32) for i in range(2)]
    ti = 0
    for c in range(NQC):
        for i in range(4):
            mm(k, psB[:, i * 32:(i + 1) * 32], qf[:, (4 * c + i) * 128:(4 * c + i + 1) * 128], kmean[:, :], True, True,
               [qf, kmean], [psB])
        for i in range(4):
            nv = 2 * c + i // 2
            tt = tts[ti % 8]
            bs_, m8_ = bsv[ti % 2], m8[ti % 2]
            ti += 1
            k.op("dve", lambda e: e.memset(tt[:, 64:96], BIG), [], [tt])
            if i < 2:
                k.op("dve", lambda e: e.memset(tt[:, 64 + 2 * c + 1:64 + 2 * c + 2], 0.0), [], [tt])
            if nv >= 3:
                k.op("dve", lambda e: e.memset(bs_[:, :], -1e30), [], [bs_])
                k.op("dve", lambda e: e.tensor_copy(out=bs_[:, 0:nv], in_=psB[:, i * 32:i * 32 + nv]), [psB], [bs_])
                k.op("dve", lambda e: e.max(out=m8_[:, :], in_=bs_[:, :]), [bs_], [m8_])
                k.op("dve", lambda e: e.tensor_scalar(out=tt[:, 64:64 + nv], in0=bs_[:, 0:nv], scalar1=m8_[:, 2:3],
                                                      scalar2=BIG, op0=ALU.is_ge, op1=ALU.mult), [bs_, m8_], [tt])
            k.op("pe", lambda e: e.transpose(psP[0:96, i * 128:(i + 1) * 128], tt[:, :], id_t[:, :]),
                 [tt, id_t], [psP])
        k.op("dve", lambda e: e.tensor_scalar(out=qaug[64:96, c * 512:(c + 1) * 512], in0=psP[64:96, :], scalar1=BIG,
                                              scalar2=None, op0=ALU.subtract), [psP], [qaug])
    psS = [k.ps("psS%d" % i, [128, 512], F32) for i in range(3)]
    psO = [k.ps("psO%d" % i, [128, 512], F32) for i in range(2)]
    psD = k.ps("psD", [64, 512], F32)
    pT = [k.sb("pT%d" % i, [128, 512], BF16) for i in range(3)]
    o_sb = [k.sb("o_sb%d" % i, [65, 512], F32) for i in range(2)]
    rec = [k.sb("rec%d" % i, [64, 512], F32) for i in range(2)]
    si = 0
    for c in range(NQC):
        qs = slice(c * 512, (c + 1) * 512)
        pO = psO[c % 2]
        nk = 4 * c + 4
        for kt in range(nk):
            pS, p_ = psS[si % 3], pT[si % 3]
            si += 1
            mm(k, pS[:, :], kaug[:, kt * 128:(kt + 1) * 128], qaug[:, qs], True, True, [kaug, qaug], [pS])
            k.op("act", lambda e: e.activation(out=p_[:, :], in_=pS[:, :], func=AF.Exp), [pS], [p_])
            if kt >= 4 * c:
                a = kt - 4 * c
                k.op("dve", lambda e: e.tensor_tensor(out=p_[:, :], in0=p_[:, :], in1=cm_t[:, a, :], op=ALU.mult),
                     [p_, cm_t], [p_])
            mm(k, pO[0:65, :], vaug[:, kt, :], p_[:, :], kt == 0, kt == nk - 1, [vaug, p_], [pO])
        o_, r_ = o_sb[c % 2], rec[c % 2]
        k.op("act", lambda e: e.copy(out=o_[:, :], in_=pO[0:65, :]), [pO], [o_])
        mm(k, psD[:, :], sel_t[:, :], o_[:, :], True, True, [sel_t, o_], [psD])
        k.op("dve", lambda e: e.reciprocal(out=r_[:, :], in_=psD[:, :]), [psD], [r_])
        k.op("dve", lambda e: e.tensor_tensor(out=r_[:, :], in0=r_[:, :], in1=o_[0:64, :], op=ALU.mult),
             [r_, o_], [r_])
        k.dma("sp", ycT[:, qs], r_[:, :], [r_], [], is_output=True)
    return k.finish()


def moba_consts(S=SEQ):
    half = 32
    inv = (10000.0 ** (-np.arange(half, dtype=np.float32) / half)).astype(np.float32)
    ang = np.arange(S, dtype=np.float32)[None, :] * inv[:, None]
    cos, sin = np.cos(ang).astype(np.float32), np.sin(ang).astype(np.float32)
    cosT = np.concatenate([cos, cos], 0)
    sinT = np.concatenate([-sin, sin], 0)
    blk = (np.arange(S)[None, :] // 256 == np.arange(32)[:, None]).astype(np.float32)
    sp = np.arange(4)[:, None, None] * 128 + np.arange(128)[None, :, None]
    cmask = (sp <= np.arange(512)[None, None, :]).astype(np.float32)
    return dict(cosT=np.ascontiguousarray(cosT), sinT=np.ascontiguousarray(sinT), blk=blk, cmask=cmask,
                ident=np.eye(128, dtype=np.float32))


def build_Bb(S=SEQ):
    k = K()
    G, NC_, C = 512, 8, 64
    NG = S // G
    LD = 0.6065306597126334
    dr = lambda n, s: k.dram(n, s, F32, "ExternalInput")
    rX, kX = dr("rX", [64, S + 1]), dr("kX", [64, S + 1])
    xwX, xaX, xgX = dr("xwX", [32, S + 1]), dr("xaX", [32, S + 1]), dr("xgX", [96, S + 1])
    v, vprev = dr("v", [S, 64]), dr("vprev", [S, 64])
    pvec = dr("pvec", [96, 12])
    w2h, a2h, g2h = dr("w2h", [32, 64]), dr("a2h", [32, 64]), dr("g2h", [96, 64])
    bc = dr("bc", [64, 3, 512])
    msk = dr("msk", [64, 4, 512])
    ident = dr("ident", [128, 128])
    yb = k.dram("yb", [S, 64], F32, "ExternalOutput")

    cst = lambda n, s: k.sb(n, s, F32)
    pv_t, w2_t, a2_t, g2_t = cst("pv_t", [96, 12]), cst("w2_t", [32, 64]), cst("a2_t", [32, 64]), cst("g2_t", [96, 64])
    bc_t, mk_t, id_t, one_t = cst("bc_t", [64, 3, 512]), cst("mk_t", [64, 4, 512]), cst("id_t", [128, 128]), cst("one_t", [64, 64])
    for t, src in ((pv_t, pvec), (w2_t, w2h), (a2_t, a2h), (g2_t, g2h), (bc_t, bc), (mk_t, msk), (id_t, ident)):
        k.dma("sp", t[:], src, [], [t])
    k.op("dve", lambda e: e.memset(one_t[:, :], 1.0), [], [one_t])
    MU_R, MU_K, MU_W, MU_A, MU_G, W0, A0, KK, KA, RK = range(10)
    pc = lambda j, n=64: pv_t[0:n, j:j + 1]
    SL, SU, IU, I8 = (mk_t[:, j, :] for j in range(4))
    id64 = id_t[0:64, 0:64]

    T = lambda n, p=64, w=G: k.sb(n, [p, w], F32)
    rXt, kXt, xwt, xat, xgt = T("rXt", 64, G + 1), T("kXt", 64, G + 1), T("xwt", 32, G + 1), T("xat", 32, G + 1), T("xgt", 96, G + 1)
    v_t, vp_t, tmp = T("v_t"), T("vp_t"), T("tmp", 96)
    sgw, cumS, ep, em, epr, a_t, kx, sq, k2, b_t = (T(n) for n in "sgw cumS ep em epr a_t kx sq k2 b_t".split())
    ahat, btil, ktil, rhat, rkT = (T(n) for n in "ahat btil ktil rhat rkT".split())
    Atok, nBtok, Ktok = T("Atok"), T("nBtok"), T("Ktok")
    Nb, Mb = [T("N0"), T("N1")], [T("M0"), T("M1")]
    Q, AkT, nAT, AkrT, AV, W, U0, PhiT, ZT = (T(n) for n in "Q AkT nAT AkrT AV W U0 PhiT ZT".split())
    y_t, o_t = T("y_t"), T("o_t")
    s1, s2, mean, rstd = T("s1", 64, 8), T("s2", 64, 8), T("mean", 64, 8), T("rstd", 64, 8)
    H = [T("H0", 64, 64), T("H1", 64, 64)]
    k.op("dve", lambda e: e.memset(H[0][:, :], 0.0), [], [H[0]])
    eps_b = T("eps_b", 64, 1)
    k.op("dve", lambda e: e.memset(eps_b[:, :], 64e-5), [], [eps_b])
    ps = [k.ps("ps%d" % i, [128, 512], F32) for i in range(8)]

    def tt(out, a, b, op, R, Wr):
        k.op("dve", lambda e: e.tensor_tensor(out=out, in0=a, in1=b, op=op), R, Wr)

    def ts(out, a, s1_, s2_, op0, op1, R, Wr):
        k.op("dve", lambda e: e.tensor_scalar(out=out, in0=a, scalar1=s1_, scalar2=s2_, op0=op0, op1=op1), R, Wr)

    def stt(out, a, sc, b, op0, op1, R, Wr):
        k.op("dve", lambda e: e.scalar_tensor_tensor(out=out, in0=a, scalar=sc, in1=b, op0=op0, op1=op1), R, Wr)

    def act(out, a, func, R, Wr, bias=0.0, scale=1.0):
        k.op("act", lambda e: e.activation(out=out, in_=a, func=func, bias=bias, scale=scale), R, Wr)

    def chunk_mm(bank, lhs_t, rhs_t, lhs_ap, rhs_ap, n=64):
        for c in range(NC_):
            cs = slice(c * C, (c + 1) * C)
            mm(k, bank[0:64, cs], lhs_ap(cs), rhs_ap(cs), True, True, [lhs_t, rhs_t], [bank])

    hi = 0
    for g in range(NG):
        t0 = g * G
        for t, src in ((rXt, rX), (kXt, kX), (xwt, xwX), (xat, xaX), (xgt, xgX)):
            k.dma("sp", t[:, :], src[:, t0:t0 + G + 1], [], [t])
        k.dma("sp", v_t[:, :].rearrange("p (c f) -> p c f", f=64), v[t0:t0 + G, :].rearrange("(c p) f -> p c f", p=64), [], [v_t])
        k.dma("sp", vp_t[:, :].rearrange("p (c f) -> p c f", f=64), vprev[t0:t0 + G, :].rearrange("(c p) f -> p c f", p=64), [], [vp_t])
        for t, mu, n in ((rXt, MU_R, 64), (kXt, MU_K, 64), (xwt, MU_W, 32), (xat, MU_A, 32), (xgt, MU_G, 96)):
            tt(tmp[0:n, :], t[:, 0:G], t[:, 1:G + 1], ALU.subtract, [t], [tmp])
            stt(t[:, 1:G + 1], tmp[0:n, :], pc(mu, n), t[:, 1:G + 1], ALU.mult, ALU.add, [tmp, pv_t, t], [t])
        r_, k_, xw_, xa_, xg_ = rXt[:, 1:G + 1], kXt[:, 1:G + 1], xwt[:, 1:G + 1], xat[:, 1:G + 1], xgt[:, 1:G + 1]
        tt(vp_t[:, :], vp_t[:, :], v_t[:, :], ALU.subtract, [vp_t, v_t], [vp_t])
        tt(vp_t[:, :], vp_t[:, :], bc_t[:, 0, :], ALU.mult, [vp_t, bc_t], [vp_t])
        tt(v_t[:, :], v_t[:, :], vp_t[:, :], ALU.add, [vp_t, v_t], [v_t])
        act(xw_, xw_, AF.Tanh, [xwt], [xwt])
        mm(k, ps[0][0:64, :], w2_t[:, :], xw_, True, True, [w2_t, xwt], [ps[0]])
        act(sgw[:, :], ps[0][0:64, :], AF.Sigmoid, [ps[0], pv_t], [sgw], bias=pc(W0))
        for c in range(NC_):
            cs = slice(c * C, (c + 1) * C)
            k.op("dve", lambda e: e.tensor_tensor_scan(out=cumS[:, cs], data0=one_t[:, :], data1=sgw[:, cs],
                                                       initial=0.0, op0=ALU.mult, op1=ALU.add), [sgw, one_t], [cumS])
        act(ep[:, :], cumS[:, :], AF.Exp, [cumS], [ep], scale=-LD)
        act(em[:, :], cumS[:, :], AF.Exp, [cumS], [em], scale=LD)
        tt(tmp[0:64, :], cumS[:, :], sgw[:, :], ALU.subtract, [cumS, sgw], [tmp])
        act(epr[:, :], tmp[0:64, :], AF.Exp, [tmp], [epr], scale=-LD)
        mm(k, ps[1][0:64, :], a2_t[:, :], xa_, True, True, [a2_t, xat], [ps[1]])
        act(a_t[:, :], ps[1][0:64, :], AF.Sigmoid, [ps[1], pv_t], [a_t], bias=pc(A0))
        ts(kx[:, :], k_, pc(KK), None, ALU.mult, ALU.bypass, [kXt, pv_t], [kx])
        act(sq[:, :], kx[:, :], AF.Square, [kx], [sq])
        mm(k, ps[2][0:64, :], one_t[:, :], sq[:, :], True, True, [one_t, sq], [ps[2]])
        act(sq[:, :], ps[2][0:64, :], AF.Sqrt, [ps[2]], [sq])
        ts(sq[:, :], sq[:, :], 1e-12, None, ALU.max, ALU.bypass, [sq], [sq])
        k.op("dve", lambda e: e.reciprocal(out=sq[:, :], in_=sq[:, :]), [sq], [sq])
        tt(kx[:, :], kx[:, :], sq[:, :], ALU.mult, [kx, sq], [kx])
        ts(k2[:, :], a_t[:, :], -1.0, pc(KA), ALU.add, ALU.mult, [a_t, pv_t], [k2])
        stt(k2[:, :], k2[:, :], 1.0, k_, ALU.add, ALU.mult, [k2, kXt], [k2])
        tt(b_t[:, :], kx[:, :], a_t[:, :], ALU.mult, [kx, a_t], [b_t])
        tt(ahat[:, :], kx[:, :], epr[:, :], ALU.mult, [kx, epr], [ahat])
        tt(btil[:, :], b_t[:, :], em[:, :], ALU.mult, [b_t, em], [btil])
        tt(ktil[:, :], k2[:, :], em[:, :], ALU.mult, [k2, em], [ktil])
        tt(rhat[:, :], r_, ep[:, :], ALU.mult, [rXt, ep], [rhat])
        stt(rkT[:, :], r_, pc(RK), k2[:, :], ALU.mult, ALU.mult, [rXt, pv_t, k2], [rkT])
        act(xg_, xg_, AF.Sigmoid, [xgt], [xgt])
        for bank, src_t in ((ps[3], ahat), (ps[4], btil), (ps[5], ktil)):
            for c in range(NC_):
                cs = slice(c * C, (c + 1) * C)
                k.op("pe", lambda e: e.transpose(bank[0:64, cs], src_t[:, cs], id64), [src_t, id_t], [bank])
        k.op("act", lambda e: e.copy(out=Atok[:, :], in_=ps[3][0:64, :]), [ps[3]], [Atok])
        ts(nBtok[:, :], ps[4][0:64, :], -1.0, None, ALU.mult, ALU.bypass, [ps[4]], [nBtok])
        k.op("act", lambda e: e.copy(out=Ktok[:, :], in_=ps[5][0:64, :]), [ps[5]], [Ktok])
        N, M = Nb[0], Mb[0]
        chunk_mm(ps[0], ahat, btil, lambda cs: ahat[:, cs], lambda cs: btil[:, cs])
        stt(N[:, :], ps[0][0:64, :], -1.0, SL, ALU.mult, ALU.mult, [ps[0], mk_t], [N])
        chunk_mm(ps[1], btil, ahat, lambda cs: btil[:, cs], lambda cs: ahat[:, cs])
        stt(M[:, :], ps[1][0:64, :], -1.0, SU, ALU.mult, ALU.mult, [ps[1], mk_t], [M])
        chunk_mm(ps[2], ktil, ahat, lambda cs: ktil[:, cs], lambda cs: ahat[:, cs])
        tt(AkT[:, :], ps[2][0:64, :], SU, ALU.mult, [ps[2], mk_t], [AkT])
        chunk_mm(ps[6], btil, rhat, lambda cs: btil[:, cs], lambda cs: rhat[:, cs])
        stt(nAT[:, :], ps[6][0:64, :], -1.0, IU, ALU.mult, ALU.mult, [ps[6], mk_t], [nAT])
        chunk_mm(ps[7], ktil, rhat, lambda cs: ktil[:, cs], lambda cs: rhat[:, cs])
        tt(AkrT[:, :], ps[7][0:64, :], IU, ALU.mult, [ps[7], mk_t], [AkrT])
        tt(Q[:, :], M[:, :], I8, ALU.add, [M, mk_t], [Q])
        for lvl in range(5):
            Nn, Mn = Nb[(lvl + 1) % 2], Mb[(lvl + 1) % 2]
            chunk_mm(ps[3], M, N, lambda cs: M[:, cs], lambda cs: N[:, cs])
            k.op("act", lambda e: e.copy(out=Nn[:, :], in_=ps[3][0:64, :]), [ps[3]], [Nn])
            if lvl < 4:
                chunk_mm(ps[4], N, M, lambda cs: N[:, cs], lambda cs: M[:, cs])
                k.op("act", lambda e: e.copy(out=Mn[:, :], in_=ps[4][0:64, :]), [ps[4]], [Mn])
            chunk_mm(ps[5], Nn, Q, lambda cs: Nn[:, cs], lambda cs: Q[:, cs])
            tt(Q[:, :], Q[:, :], ps[5][0:64, :], ALU.add, [Q, ps[5]], [Q])
            N, M = Nn, Mn
        chunk_mm(ps[0], AkT, v_t, lambda cs: AkT[:, cs], lambda cs: v_t[:, cs])
        k.op("act", lambda e: e.copy(out=AV[:, :], in_=ps[0][0:64, :]), [ps[0]], [AV])
        chunk_mm(ps[1], Q, Atok, lambda cs: Q[:, cs], lambda cs: Atok[:, cs])
        k.op("act", lambda e: e.copy(out=W[:, :], in_=ps[1][0:64, :]), [ps[1]], [W])
        chunk_mm(ps[2], Q, AV, lambda cs: Q[:, cs], lambda cs: AV[:, cs])
        k.op("act", lambda e: e.copy(out=U0[:, :], in_=ps[2][0:64, :]), [ps[2]], [U0])
        chunk_mm(ps[6], W, nBtok, lambda cs: W[:, cs], lambda cs: nBtok[:, cs])
        tt(PhiT[:, :], ps[6][0:64, :], I8, ALU.add, [ps[6], mk_t], [PhiT])
        chunk_mm(ps[7], W, nAT, lambda cs: W[:, cs], lambda cs: nAT[:, cs])
        tt(ZT[:, :], ps[7][0:64, :], rhat[:, :], ALU.add, [ps[7], rhat], [ZT])
        for c in range(NC_):
            cs = slice(c * C, (c + 1) * C)
            Hc, Hn = H[hi % 2], H[(hi + 1) % 2]
            pX = ps[4][0:64, (hi % 2) * 64:(hi % 2) * 64 + 64]
            hi += 1
            pY = ps[3][0:64, cs]
            mm(k, pY, AkrT[:, cs], v_t[:, cs], True, False, [AkrT, v_t], [ps[3]])
            mm(k, pY, nAT[:, cs], U0[:, cs], False, False, [nAT, U0], [ps[3]])
            mm(k, pX, Ktok[:, cs], v_t[:, cs], True, False, [Ktok, v_t], [ps[4]])
            mm(k, pX, nBtok[:, cs], U0[:, cs], False, False, [nBtok, U0], [ps[4]])
            mm(k, pX, PhiT[:, cs], Hc[:, :], False, True, [PhiT, Hc], [ps[4]])
            mm(k, pY, ZT[:, cs], Hc[:, :], False, True, [ZT, Hc], [ps[3]])
            k.op("act", lambda e: e.activation(out=Hn[:, :], in_=pX, func=AF.Copy, scale=ep[:, c * C + 63:c * C + 64]),
                 [ps[4], ep], [Hn])
        k.op("act", lambda e: e.copy(out=y_t[:, :], in_=ps[3][0:64, :]), [ps[3]], [y_t])
        v3 = lambda t_: t_[:, :].rearrange("p (c f) -> p c f", f=64)
        k.op("dve", lambda e: e.reduce_sum(out=s1[:, :], in_=v3(y_t), axis=AX.X), [y_t], [s1])
        act(sq[:, :], y_t[:, :], AF.Square, [y_t], [sq])
        k.op("dve", lambda e: e.reduce_sum(out=s2[:, :], in_=v3(sq), axis=AX.X), [sq], [s2])
        ts(mean[:, :], s1[:, :], 1.0 / 64, None, ALU.mult, ALU.bypass, [s1], [mean])
        tt(s1[:, :], mean[:, :], mean[:, :], ALU.mult, [mean], [s1])
        stt(rstd[:, :], s2[:, :], 1.0 / 64, s1[:, :], ALU.mult, ALU.subtract, [s2, s1], [rstd])
        act(rstd[:, :], rstd[:, :], AF.Sqrt, [rstd, eps_b], [rstd], bias=eps_b[:, 0:1])
        k.op("dve", lambda e: e.reciprocal(out=rstd[:, :], in_=rstd[:, :]), [rstd], [rstd])
        for c in range(NC_):
            cs = slice(c * C, (c + 1) * C)
            ts(y_t[:, cs], y_t[:, cs], mean[:, c:c + 1], rstd[:, c:c + 1], ALU.subtract, ALU.mult, [y_t, mean, rstd], [y_t])
        tt(y_t[:, :], y_t[:, :], bc_t[:, 1, :], ALU.mult, [y_t, bc_t], [y_t])
        tt(y_t[:, :], y_t[:, :], bc_t[:, 2, :], ALU.add, [y_t, bc_t], [y_t])
        chunk_mm(ps[5], rkT, one_t, lambda cs: rkT[:, cs], lambda cs: one_t[:, :])
        tt(o_t[:, :], ps[5][0:64, :], v_t[:, :], ALU.mult, [ps[5], v_t], [o_t])
        tt(y_t[:, :], y_t[:, :], o_t[:, :], ALU.add, [y_t, o_t], [y_t])
        chunk_mm(ps[6], xgt, g2_t, lambda cs: xgt[:, 1 + cs.start:1 + cs.stop], lambda cs: g2_t[:, :])
        tt(o_t[:, :], y_t[:, :], ps[6][0:64, :], ALU.mult, [y_t, ps[6]], [o_t])
        k.dma("sp", yb[t0:t0 + G, :].rearrange("(c p) f -> p c f", p=64), v3(o_t), [o_t], [], is_output=True)
    return k.finish()


def rwkv_consts():
    e = np.ones((64, 64), np.float32)
    SL = np.tril(e, -1)
    SU = np.triu(e, 1)
    IU = np.triu(e)
    I = np.eye(64, dtype=np.float32)
    msk = np.stack([np.tile(m, (1, 8)) for m in (SL, SU, IU, I)], 1)
    return dict(msk=np.ascontiguousarray(msk), ident=np.eye(128, dtype=np.float32))


def rwkv_inputs(pb, prm, h):
    S = pb.shape[0]
    hs = slice(h * 64, (h + 1) * 64)

    def fmx(a):
        out = np.zeros((a.shape[1], S + 1), np.float32)
        out[:, 1:] = a.T
        return out
    mu = prm["b_mu"]
    v = np.ascontiguousarray(pb[:, 1024:1536][:, hs])
    vprev = np.zeros_like(v)
    vprev[1:] = v[:-1]
    pvec = np.zeros((96, 12), np.float32)
    pvec[:64, 0] = mu[0:512][hs]
    pvec[:64, 1] = mu[512:1024][hs]
    pvec[:32, 2] = mu[1536:1568]
    pvec[:32, 3] = mu[1568:1600]
    pvec[:96, 4] = mu[1600:1696]
    pvec[:64, 5] = prm["b_w0"][hs]
    pvec[:64, 6] = prm["b_a0"][hs]
    pvec[:64, 7] = prm["b_kk"][hs]
    pvec[:64, 8] = prm["b_ka"][hs]
    pvec[:64, 9] = prm["b_rk"][h]
    bc = np.stack([np.tile(np.broadcast_to(a[hs], (64, 64)), (1, 8)) for a in
                   (mu[1024:1536], prm["b_lnx_g"], prm["b_lnx_b"])], 1)
    return dict(rX=fmx(pb[:, 0:512][:, hs]), kX=fmx(pb[:, 512:1024][:, hs]), xwX=fmx(pb[:, 1536:1568]),
                xaX=fmx(pb[:, 1568:1600]), xgX=fmx(pb[:, 1600:1696]), v=v, vprev=vprev, pvec=pvec,
                w2h=np.ascontiguousarray(prm["b_w2"][:, hs]), a2h=np.ascontiguousarray(prm["b_a2"][:, hs]),
                g2h=np.ascontiguousarray(prm["b_g2"][:, hs]), bc=np.ascontiguousarray(bc.astype(np.float32)),
                **rwkv_consts())


_CACHE = {}


def _prog(name, fn):
    if name not in _CACHE:
        _CACHE[name] = fn()
    return _CACHE[name]


def _run(nc, ims):
    return run_bass_kernel_spmd(nc, ims, core_ids=list(range(NCORES))).results


def _fm(vec):
    return np.ascontiguousarray(np.asarray(vec, np.float32).reshape(-1, 128).T)


def kernel(x, mix_norm_g, w_in, a_ln_g, a_ln_b, a_ws, a_bs, b_mu, b_w0, b_w2, b_a0, b_a2,
           b_g2, b_kk, b_ka, b_rk, b_lnx_g, b_lnx_b, d_gate_w2, d_gate_b, d_norm_g, w_out,
           ffn_norm_g, w_up, conv_w, conv_b, w_down, final_norm_g):
    f = lambda a: np.asarray(a, np.float32)
    xs = f(x)[0]
    S = xs.shape[0]
    TC = S // NCORES
    NU, U = 2, TC // 2
    eye = np.eye(128, dtype=np.float32)
    mcst = moba_consts(S)
    for l in range(DEPTH):
        ncA = _prog("A", build_A)
        ims = [dict(xT=np.ascontiguousarray(xs[c * TC:(c + 1) * TC].T), g=_fm(f(mix_norm_g)[l]), w=f(w_in)[l])
               for c in range(NCORES)]
        res = _run(ncA, ims)
        p = np.concatenate([r["pT"].T for r in res], 0)
        pa, pb, pc, pd = np.split(p, [1024, 1024 + 1696, 1024 + 1696 + 1536], axis=-1)
        ncBa = _prog("Ba", build_Ba)
        ims = []
        for c in range(NCORES):
            sl = slice(c * TC, (c + 1) * TC)
            ims.append(dict(puT=np.ascontiguousarray(pa[sl, :512].T), pv=np.ascontiguousarray(pa[sl, 512:]),
                            lng=np.ascontiguousarray(np.broadcast_to(f(a_ln_g)[l], (128, 512))),
                            lnb=np.ascontiguousarray(np.broadcast_to(f(a_ln_b)[l], (128, 512))),
                            ws=f(a_ws)[l], bsb=np.ascontiguousarray(np.broadcast_to(f(a_bs)[l][None], (128, 4, 128))),
                            triu=np.triu(np.ones((128, 128), np.float32)), ident=eye))
        res = _run(ncBa, ims)
        ya = np.concatenate([r["yaT"].T for r in res], 0)
        ncBb = _prog("Bb", build_Bb)
        prm = dict(b_mu=f(b_mu)[l], b_w0=f(b_w0)[l], b_w2=f(b_w2)[l], b_a0=f(b_a0)[l], b_a2=f(b_a2)[l], b_g2=f(b_g2)[l],
                   b_kk=f(b_kk)[l], b_ka=f(b_ka)[l], b_rk=f(b_rk)[l], b_lnx_g=f(b_lnx_g)[l], b_lnx_b=f(b_lnx_b)[l])
        res = _run(ncBb, [rwkv_inputs(pb, prm, c) for c in range(NCORES)])
        yb = np.concatenate([r["yb"] for r in res], 1)
        ncBc = _prog("Bc", build_Bc)
        q, kk_, v_ = np.split(pc, 3, axis=-1)
        sw = lambda a: np.concatenate([a[:, 32:], a[:, :32]], 1)
        ims = []
        for c in range(NCORES):
            hs = slice(c * 64, (c + 1) * 64)
            ims.append(dict(qA=np.ascontiguousarray(q[:, hs].T), qB=np.ascontiguousarray(sw(q[:, hs]).T),
                            kA=np.ascontiguousarray(kk_[:, hs].T), kB=np.ascontiguousarray(sw(kk_[:, hs]).T),
                            v=np.ascontiguousarray(v_[:, hs]), **mcst))
        res = _run(ncBc, ims)
        yc = np.concatenate([r["ycT"].T for r in res], 1)
        ncBd = _prog("Bd", build_Bd)
        gq, gk, gv, gxg, gog = np.split(pd, [256, 512, 1024, 1040], axis=-1)
        ims = []
        for c in range(NCORES):
            h = c % 4
            ims.append(dict(qT=np.ascontiguousarray(gq[:, h * 64:(h + 1) * 64].T),
                            kT=np.ascontiguousarray(gk[:, h * 64:(h + 1) * 64].T),
                            v=np.ascontiguousarray(gv[:, h * 128:(h + 1) * 128]),
                            og=np.ascontiguousarray(gog[:, h * 128:(h + 1) * 128]),
                            xgT=np.ascontiguousarray(gxg.T),
                            w2h=np.ascontiguousarray(f(d_gate_w2)[l][:, h * 64:(h + 1) * 64]),
                            gb=np.ascontiguousarray(f(d_gate_b)[l][h * 64:(h + 1) * 64, None]),
                            ngb=np.ascontiguousarray(np.broadcast_to(f(d_norm_g)[l], (64, 128))), ident=eye,
                            maskU=np.ascontiguousarray(np.tile(np.triu(np.ones((64, 64), np.float32)), (1, 8)))))
        res = _run(ncBd, ims)
        yd = np.concatenate([res[c]["yd"] for c in range(4)], 1)
        y = np.concatenate([ya, yb, yc, yd], 1)
        final = l == DEPTH - 1
        ncC = _prog("C%d" % final, lambda: build_C(final, NU=NU, U=U))
        xp = np.concatenate([np.zeros((2, D), np.float32), xs], 0)
        yp = np.concatenate([np.zeros((2, D), np.float32), y], 0)
        units = lambda a, c: np.ascontiguousarray(
            np.stack([a[c * TC + u * U:c * TC + u * U + U + 2].T for u in range(NU)]))
        cw = np.ascontiguousarray(f(conv_w)[l].T.reshape(-1, 128, 3).transpose(1, 0, 2))
        ims = [dict(xT=units(xp, c), yT=units(yp, c), wo=f(w_out)[l], g2=_fm(f(ffn_norm_g)[l]), wu=f(w_up)[l], cw=cw,
                    cb=_fm(f(conv_b)[l]), wd=f(w_down)[l], gf=_fm(f(final_norm_g))) for c in range(NCORES)]
        res = _run(ncC, ims)
        xs = np.concatenate([np.concatenate([r["xo"][u].T for u in range(NU)], 0) for r in res], 0)
    return xs[None].astype(np.float32)
```
